# Optimizing a Trainium2 kernel written in Bass

```python
import jax
import jax.numpy as jnp
from jax import lax
import numpy as np

D_MODEL = 1024
BATCH = 16
SEQ = 4096
DEPTH = 2

GRID_W = 64
CTX_LEN = 256
BRANCH_W = 512
N_BRANCH = 3
GLA_H = 4
GLA_DK = 64
GLA_DV = 128
GLA_LR = 16
GLA_TAU = 16.0
GLA_CHUNK = 64
ATT_H = 4
ATT_KV = 2
ATT_G = ATT_H // ATT_KV
ATT_HD = 128
Q_BLOCK = 128
ROPE_THETA = 10000.0
RWKV_H = 8
RWKV_HD = 64
RWKV_W_LR = 64
RWKV_A_LR = 64
RWKV_DECAY_SCALE = 0.6065306597
NORM_EPS = 1e-6
GN_EPS = 64e-5
L2_EPS = 1e-12
F32 = jnp.float32

GLA_COLS = (GLA_H * GLA_DK, GLA_H * GLA_DK, GLA_H * GLA_DV, BRANCH_W, GLA_LR, GLA_LR)
ATT_COLS = (ATT_H * ATT_HD, ATT_KV * ATT_HD, ATT_KV * ATT_HD, BRANCH_W)
RWKV_COLS = (BRANCH_W, BRANCH_W, BRANCH_W, BRANCH_W, RWKV_W_LR, RWKV_W_LR, RWKV_A_LR, RWKV_A_LR)
MERGE_COLS = (D_MODEL, D_MODEL, D_MODEL)
GROUP_COLS = (sum(GLA_COLS), sum(ATT_COLS), sum(RWKV_COLS), sum(MERGE_COLS))
D_IN = sum(GROUP_COLS)

kernel_name = 'hybrid_gla_gqa_rwkv7_prefix_dit'


def _offsets(sizes):
    out, acc = [], 0
    for s in sizes:
        out.append((acc, acc + s))
        acc += s
    return out


def _split(p, sizes):
    return [p[..., a:b] for a, b in _offsets(sizes)]


def _project(h, w, sizes):
    return [h @ w[:, a:b] for a, b in _offsets(sizes)]


def rms_norm(x, g):
    xf = x.astype(F32)
    y = xf * lax.rsqrt(jnp.mean(xf * xf, -1, keepdims=True) + NORM_EPS)
    return (y * g.astype(F32)).astype(x.dtype)


def head_layer_norm(y, g, b):
    mu = jnp.mean(y, -1, keepdims=True)
    d = y - mu
    var = jnp.mean(d * d, -1, keepdims=True)
    return d * lax.rsqrt(var + GN_EPS) * g + b


def modulation(cond, w_mod, b_mod):
    m = jax.nn.silu(cond) @ w_mod + b_mod
    return jnp.split(m, 3, axis=-1)


def centred_shift(p, mu):
    pad = jnp.pad(p, ((0, 0), (1, 1), (0, 0)))
    nb = 0.5 * (pad[:, :-2] + pad[:, 2:])
    return p + mu * (nb - p)


def axial_rope(t, row_pos, col_pos):
    half = ATT_HD // 2
    quarter = half // 2
    inv = ROPE_THETA ** (-jnp.arange(quarter, dtype=F32) / quarter)
    tf = t.astype(F32)

    def rot(u, pos):
        ang = pos[:, None] * inv[None, :]
        cos = jnp.cos(ang)[None, :, None, :]
        sin = jnp.sin(ang)[None, :, None, :]
        u1, u2 = u[..., :quarter], u[..., quarter:]
        return jnp.concatenate([u1 * cos - u2 * sin, u2 * cos + u1 * sin], -1)

    return jnp.concatenate([rot(tf[..., :half], row_pos), rot(tf[..., half:], col_pos)], -1).astype(t.dtype)


def gla_chunked(q, k, v, g, s0):
    b, h, t, dk = q.shape
    dv = v.shape[-1]
    n = t // GLA_CHUNK
    q = q.reshape(b, h, n, GLA_CHUNK, dk)
    k = k.reshape(b, h, n, GLA_CHUNK, dk)
    v = v.reshape(b, h, n, GLA_CHUNK, dv)
    g_cum = jnp.cumsum(g.reshape(b, h, n, GLA_CHUNK, dk), axis=3)
    g_last = g_cum[:, :, :, -1:, :]
    q_dec = q * jnp.exp(g_cum)
    k_inv = k * jnp.exp(-g_cum)
    k_tail = k * jnp.exp(g_last - g_cum)
    lower = jnp.tril(jnp.ones((GLA_CHUNK, GLA_CHUNK), bool))
    a = jnp.where(lower, jnp.einsum('bhncd,bhnsd->bhncs', q_dec, k_inv), 0.0)
    o = jnp.einsum('bhncs,bhnse->bhnce', a, v)
    u = jnp.einsum('bhncd,bhnce->bhnde', k_tail, v)
    decay = jnp.exp(g_last[:, :, :, 0, :])

    def step(s, inp):
        d_n, u_n = inp
        return d_n[..., None] * s + u_n, s

    s_fin, s_in = lax.scan(step, s0, (jnp.moveaxis(decay, 2, 0), jnp.moveaxis(u, 2, 0)))
    o = o + jnp.einsum('bhncd,nbhde->bhnce', q_dec, s_in)
    return o.reshape(b, h, t, dv), s_fin


def gla_prep(p, wup_f, b_f, wup_b, b_b):
    q, k, v, gate, wd_f, wd_b = _split(p.astype(F32), GLA_COLS)
    b, t, _ = q.shape

    def heads(z, d):
        return z.reshape(b, t, GLA_H, d).transpose(0, 2, 1, 3)

    g_f = jax.nn.log_sigmoid(wd_f @ wup_f + b_f) / GLA_TAU
    g_b = jax.nn.log_sigmoid(wd_b @ wup_b + b_b) / GLA_TAU
    return (heads(q * GLA_DK ** -0.5, GLA_DK), heads(k, GLA_DK), heads(v, GLA_DV),
            heads(g_f, GLA_DK), heads(g_b, GLA_DK), gate)


def gla_out(o, gate, g_norm, dtype):
    o = o.transpose(0, 2, 1, 3)
    b, t = o.shape[:2]
    o = rms_norm(o, g_norm).reshape(b, t, GLA_H * GLA_DV)
    return (o * jax.nn.silu(gate)).astype(dtype)


def gla_branch(p_lat, p_ctx, wup_f, b_f, wup_b, b_b, g_norm, need_ctx):
    ql, kl, vl, gfl, gbl, gate_l = gla_prep(p_lat, wup_f, b_f, wup_b, b_b)
    qc, kc, vc, gfc, gbc, gate_c = gla_prep(p_ctx, wup_f, b_f, wup_b, b_b)
    s0 = jnp.zeros((ql.shape[0], GLA_H, GLA_DK, GLA_DV), F32)

    def fl(z):
        return z[:, :, ::-1]

    oc_f, sc_f = gla_chunked(qc, kc, vc, gfc, s0)
    oc_b, sc_b = gla_chunked(fl(qc), fl(kc), fl(vc), fl(gbc), s0)
    ol_f, _ = gla_chunked(ql, kl, vl, gfl, sc_f)
    ol_b, _ = gla_chunked(fl(ql), fl(kl), fl(vl), fl(gbl), sc_b)
    y_lat = gla_out(ol_f + fl(ol_b), gate_l, g_norm, p_lat.dtype)
    y_ctx = gla_out(oc_f + fl(oc_b), gate_c, g_norm, p_ctx.dtype) if need_ctx else None
    return y_lat, y_ctx


def attn_prep(p, qn_g, kn_g):
    q, k, v, gate = _split(p, ATT_COLS)
    b, t, _ = q.shape
    q = rms_norm(q.reshape(b, t, ATT_H, ATT_HD), qn_g)
    k = rms_norm(k.reshape(b, t, ATT_KV, ATT_HD), kn_g)
    return q, k, v.reshape(b, t, ATT_KV, ATT_HD), gate


def attend(qg, k, v):
    s = jnp.einsum('bqkgd,bskd->bkgqs', qg, k) * (ATT_HD ** -0.5)
    pr = jax.nn.softmax(s.astype(F32), axis=-1).astype(v.dtype)
    return jnp.einsum('bkgqs,bskd->bqkgd', pr, v)


def attn_branch(p_lat, p_ctx, qn_g, kn_g, row_pos, col_pos, need_ctx):
    ql, kl, vl, gate_l = attn_prep(p_lat, qn_g, kn_g)
    qc, kc, vc, gate_c = attn_prep(p_ctx, qn_g, kn_g)
    ql = axial_rope(ql, row_pos, col_pos)
    kl = axial_rope(kl, row_pos, col_pos)
    b, t = ql.shape[:2]
    k_all = jnp.concatenate([kl, kc], axis=1)
    v_all = jnp.concatenate([vl, vc], axis=1)
    qg = ql.reshape(b, t // Q_BLOCK, Q_BLOCK, ATT_KV, ATT_G, ATT_HD).swapaxes(0, 1)
    o = lax.map(lambda qb: attend(qb, k_all, v_all), qg)
    o = o.swapaxes(0, 1).reshape(b, t, ATT_H * ATT_HD)
    y_lat = o * jax.nn.silu(gate_l)
    y_ctx = None
    if need_ctx:
        lc = qc.shape[1]
        oc = attend(qc.reshape(b, lc, ATT_KV, ATT_G, ATT_HD), kc, vc).reshape(b, lc, ATT_H * ATT_HD)
        y_ctx = oc * jax.nn.silu(gate_c)
    return y_lat, y_ctx


def rwkv_scan(r, w, k, v, kk, a, s0, reverse):
    xs = tuple(jnp.moveaxis(z, 1, 0) for z in (r, w, k, v, kk, a))

    def step(s, inp):
        r_t, w_t, k_t, v_t, kk_t, a_t = inp
        sa = jnp.einsum('bhvk,bhk->bhv', s, kk_t)
        s = (s * w_t[:, :, None, :] - sa[..., None] * (kk_t * a_t)[:, :, None, :]
             + v_t[..., None] * k_t[:, :, None, :])
        return s, jnp.einsum('bhvk,bhk->bhv', s, r_t)

    s_fin, y = lax.scan(step, s0, xs, reverse=reverse)
    return jnp.moveaxis(y, 0, 1), s_fin


def rwkv_prep(p, mu, w0_f, wup_f, w0_b, wup_b, a0_f, aup_f, a0_b, aup_b, k_k, k_a):
    p = centred_shift(p, mu).astype(F32)
    r, k, v, gate, wd_f, wd_b, ad_f, ad_b = _split(p, RWKV_COLS)
    b, t, _ = r.shape

    def heads(z):
        return z.reshape(b, t, RWKV_H, RWKV_HD)

    def direction(w0, wup, a0, aup, wd, ad):
        w = jnp.exp(-RWKV_DECAY_SCALE * jax.nn.sigmoid(w0 + jnp.tanh(wd) @ wup))
        a = jax.nn.sigmoid(a0 + ad @ aup)
        kd = k * (1.0 + (a - 1.0) * k_a)
        return heads(w), heads(kd), heads(a)

    kk = heads(k * k_k)
    kk = kk * lax.rsqrt(jnp.sum(kk * kk, -1, keepdims=True) + L2_EPS)
    return (heads(r), heads(v), kk, direction(w0_f, wup_f, a0_f, aup_f, wd_f, ad_f),
            direction(w0_b, wup_b, a0_b, aup_b, wd_b, ad_b), gate)


def rwkv_branch(p_lat, p_ctx, mu, w0_f, wup_f, w0_b, wup_b, a0_f, aup_f, a0_b, aup_b,
                k_k, k_a, r_k, ln_g, ln_b, need_ctx):
    prm = (mu, w0_f, wup_f, w0_b, wup_b, a0_f, aup_f, a0_b, aup_b, k_k, k_a)
    rl, vl, kkl, (wfl, kfl, afl), (wbl, kbl, abl), gate_l = rwkv_prep(p_lat, *prm)
    rc, vc, kkc, (wfc, kfc, afc), (wbc, kbc, abc), gate_c = rwkv_prep(p_ctx, *prm)
    s0 = jnp.zeros((rl.shape[0], RWKV_H, RWKV_HD, RWKV_HD), F32)
    yc_f, sc_f = rwkv_scan(rc, wfc, kfc, vc, kkc, afc, s0, False)
    yc_b, sc_b = rwkv_scan(rc, wbc, kbc, vc, kkc, abc, s0, True)
    yl_f, _ = rwkv_scan(rl, wfl, kfl, vl, kkl, afl, sc_f, False)
    yl_b, _ = rwkv_scan(rl, wbl, kbl, vl, kkl, abl, sc_b, True)

    def out(y, r, kf, kb, v, gate, dtype):
        y = head_layer_norm(y, ln_g, ln_b) + jnp.sum(r * (kf + kb) * r_k, -1, keepdims=True) * v
        b, t = y.shape[:2]
        return (y.reshape(b, t, BRANCH_W) * jax.nn.silu(gate)).astype(dtype)

    y_lat = out(yl_f + yl_b, rl, kfl, kbl, vl, gate_l, p_lat.dtype)
    y_ctx = out(yc_f + yc_b, rc, kfc, kbc, vc, gate_c, p_ctx.dtype) if need_ctx else None
    return y_lat, y_ctx


def layer(x, xc, c, c_ctx, row_pos, col_pos, need_ctx, w_mod, b_mod, g_pre, w_in,
          gla_wup_f, gla_b_f, gla_wup_b, gla_b_b, gla_norm, att_qnorm, att_knorm,
          rwkv_mu, rwkv_w0_f, rwkv_wup_f, rwkv_w0_b, rwkv_wup_b, rwkv_a0_f, rwkv_aup_f,
          rwkv_a0_b, rwkv_aup_b, rwkv_kk, rwkv_ka, rwkv_rk, rwkv_ln_g, rwkv_ln_b,
          w_o_gla, w_o_att, w_o_rwkv, w_out, g_post):
    shift, scale, gate = modulation(c, w_mod, b_mod)
    shift_c, scale_c, gate_c = modulation(c_ctx, w_mod, b_mod)
    h = rms_norm(x, g_pre) * (1.0 + scale[:, None]) + shift[:, None]
    hc = rms_norm(xc, g_pre) * (1.0 + scale_c) + shift_c
    gla_l, att_l, rwkv_l, mg_l = _project(h, w_in, GROUP_COLS)
    ctx_parts = _project(hc, w_in, GROUP_COLS if need_ctx else GROUP_COLS[:3])
    gla_c, att_c, rwkv_c = ctx_parts[0], ctx_parts[1], ctx_parts[2]

    y1, y1c = gla_branch(gla_l, gla_c, gla_wup_f, gla_b_f, gla_wup_b, gla_b_b, gla_norm, need_ctx)
    y2, y2c = attn_branch(att_l, att_c, att_qnorm, att_knorm, row_pos, col_pos, need_ctx)
    y3, y3c = rwkv_branch(rwkv_l, rwkv_c, rwkv_mu, rwkv_w0_f, rwkv_wup_f, rwkv_w0_b, rwkv_wup_b,
                          rwkv_a0_f, rwkv_aup_f, rwkv_a0_b, rwkv_aup_b, rwkv_kk, rwkv_ka,
                          rwkv_rk, rwkv_ln_g, rwkv_ln_b, need_ctx)

    def merge(ya, yb, yc, mg):
        g1, g2, g3 = _split(mg, MERGE_COLS)
        m = (jax.nn.sigmoid(g1) * (ya @ w_o_gla) + jax.nn.sigmoid(g2) * (yb @ w_o_att)
             + jax.nn.sigmoid(g3) * (yc @ w_o_rwkv))
        return rms_norm(m @ w_out, g_post)

    x = x + gate[:, None] * merge(y1, y2, y3, mg_l)
    xc_new = xc + gate_c * merge(y1c, y2c, y3c, ctx_parts[3]) if need_ctx else None
    return x, xc_new


def setup_inputs(seed: int = 0) -> dict:
    key = jax.random.key(seed)
    ks = iter(jax.random.split(key, 48))
    L, D = DEPTH, D_MODEL

    def nrm(shape, s):
        return s * jax.random.normal(next(ks), shape, F32)

    return {
        'x': nrm((BATCH, SEQ, D), 1.0),
        'c': nrm((BATCH, D), 1.0),
        'ctx': nrm((BATCH, CTX_LEN, D), 1.0),
        'c_ctx': nrm((D,), 1.0),
        'w_mod': nrm((L, D, 3 * D), 0.5 * D ** -0.5),
        'b_mod': nrm((L, 3 * D), 0.02),
        'g_pre': 1.0 + nrm((L, D), 0.05),
        'w_in': nrm((L, D, D_IN), D ** -0.5),
        'gla_wup_f': nrm((L, GLA_LR, GLA_H * GLA_DK), GLA_LR ** -0.5),
        'gla_b_f': nrm((L, GLA_H * GLA_DK), 0.1),
        'gla_wup_b': nrm((L, GLA_LR, GLA_H * GLA_DK), GLA_LR ** -0.5),
        'gla_b_b': nrm((L, GLA_H * GLA_DK), 0.1),
        'gla_norm': 1.0 + nrm((L, GLA_H, GLA_DV), 0.05),
        'att_qnorm': 1.0 + nrm((L, ATT_HD), 0.05),
        'att_knorm': 1.0 + nrm((L, ATT_HD), 0.05),
        'rwkv_mu': jax.random.uniform(next(ks), (L, GROUP_COLS[2]), F32),
        'rwkv_w0_f': nrm((L, BRANCH_W), 0.5),
        'rwkv_wup_f': nrm((L, RWKV_W_LR, BRANCH_W), 0.5 * RWKV_W_LR ** -0.5),
        'rwkv_w0_b': nrm((L, BRANCH_W), 0.5),
        'rwkv_wup_b': nrm((L, RWKV_W_LR, BRANCH_W), 0.5 * RWKV_W_LR ** -0.5),
        'rwkv_a0_f': nrm((L, BRANCH_W), 0.1),
        'rwkv_aup_f': nrm((L, RWKV_A_LR, BRANCH_W), 0.5 * RWKV_A_LR ** -0.5),
        'rwkv_a0_b': nrm((L, BRANCH_W), 0.1),
        'rwkv_aup_b': nrm((L, RWKV_A_LR, BRANCH_W), 0.5 * RWKV_A_LR ** -0.5),
        'rwkv_kk': 0.85 + nrm((L, BRANCH_W), 0.05),
        'rwkv_ka': 1.0 + nrm((L, BRANCH_W), 0.05),
        'rwkv_rk': nrm((L, RWKV_H, RWKV_HD), 0.1),
        'rwkv_ln_g': 1.0 + nrm((L, RWKV_H, RWKV_HD), 0.05),
        'rwkv_ln_b': nrm((L, RWKV_H, RWKV_HD), 0.02),
        'w_o_gla': nrm((L, BRANCH_W, D), BRANCH_W ** -0.5),
        'w_o_att': nrm((L, BRANCH_W, D), BRANCH_W ** -0.5),
        'w_o_rwkv': nrm((L, BRANCH_W, D), BRANCH_W ** -0.5),
        'w_out': nrm((L, D, D), D ** -0.5),
        'g_post': 1.0 + nrm((L, D), 0.05),
    }


def reference(x, c, ctx, c_ctx, w_mod, b_mod, g_pre, w_in, gla_wup_f, gla_b_f, gla_wup_b,
              gla_b_b, gla_norm, att_qnorm, att_knorm, rwkv_mu, rwkv_w0_f, rwkv_wup_f,
              rwkv_w0_b, rwkv_wup_b, rwkv_a0_f, rwkv_aup_f, rwkv_a0_b, rwkv_aup_b, rwkv_kk,
              rwkv_ka, rwkv_rk, rwkv_ln_g, rwkv_ln_b, w_o_gla, w_o_att, w_o_rwkv, w_out, g_post):
    n_lat = x.shape[1]
    ROWS = n_lat // GRID_W
    row_pos = jnp.repeat(jnp.arange(ROWS, dtype=F32), GRID_W)
    col_pos = jnp.tile(jnp.arange(GRID_W, dtype=F32), ROWS)
    xc = ctx
    for l in range(DEPTH):
        x, xc = layer(x, xc, c, c_ctx, row_pos, col_pos, l < DEPTH - 1,
                      w_mod[l], b_mod[l], g_pre[l], w_in[l],
                      gla_wup_f[l], gla_b_f[l], gla_wup_b[l], gla_b_b[l], gla_norm[l],
                      att_qnorm[l], att_knorm[l],
                      rwkv_mu[l], rwkv_w0_f[l], rwkv_wup_f[l], rwkv_w0_b[l], rwkv_wup_b[l],
                      rwkv_a0_f[l], rwkv_aup_f[l], rwkv_a0_b[l], rwkv_aup_b[l], rwkv_kk[l],
                      rwkv_ka[l], rwkv_rk[l], rwkv_ln_g[l], rwkv_ln_b[l],
                      w_o_gla[l], w_o_att[l], w_o_rwkv[l], w_out[l], g_post[l])
    return x
```

```python
import numpy as np
from contextlib import ExitStack
import concourse.bass as bass
import concourse.mybir as mybir
from concourse.bass_utils import run_bass_kernel_spmd

F32 = mybir.dt.float32
BF16 = mybir.dt.bfloat16
ALU = mybir.AluOpType
AF = mybir.ActivationFunctionType
AX = mybir.AxisListType
ENGS = ("tensor", "vector", "scalar", "gpsimd", "sync")
NPOOL = 28

NT = 4352
NLAT = 4096
NCH = 68
D = 1024
DIN = 8480
C_GLA, C_ATT, C_RW, C_MG = 0, 1568, 3104, 5408
S_RW = -0.6065306597
S_GLA = 1.0 / 16.0
_STATS = {}


class Buf:
    __slots__ = ("name", "st")

    def __init__(self, name):
        self.name = name
        self.st = {}


class Op:
    __slots__ = ("eng", "fn", "deps", "is_dma", "needs_inc", "semval", "dma_n", "seq")


class Prog:
    def __init__(self, nc):
        self.nc = nc
        self.ops = {e: [] for e in ENGS}
        self.ndma = 0
        self.dma_ops = []
        self.es = ExitStack()
        self.last_comp = {e: None for e in ENGS}

    def _state(self, buf, part):
        s = buf.st.get(part)
        if s is None:
            s = [None, []]
            buf.st[part] = s
        return s

    def _conf(self, buf, part):
        if part is None:
            if None not in buf.st:
                self._state(buf, None)
            return list(buf.st.values())
        out = [self._state(buf, part)]
        if None in buf.st:
            out.append(buf.st[None])
        return out

    def op(self, eng, fn, reads=(), writes=(), dma=False):
        o = Op()
        o.eng, o.fn, o.is_dma, o.needs_inc, o.semval, o.dma_n = eng, fn, dma, False, None, None
        deps = set()
        for t in reads:
            buf, part = t if isinstance(t, tuple) else (t, None)
            for s in self._conf(buf, part):
                if s[0] is not None:
                    deps.add(s[0])
        for t in writes:
            buf, part = t if isinstance(t, tuple) else (t, None)
            for s in self._conf(buf, part):
                if s[0] is not None:
                    deps.add(s[0])
                deps.update(s[1])
        if dma:
            o.dma_n = self.ndma
            self.ndma += 1
            if o.dma_n >= NPOOL:
                deps.add(self.dma_ops[o.dma_n - NPOOL])
            self.dma_ops.append(o)
        for t in reads:
            buf, part = t if isinstance(t, tuple) else (t, None)
            self._state(buf, part)[1].append(o)
        for t in writes:
            buf, part = t if isinstance(t, tuple) else (t, None)
            if part is None:
                buf.st = {None: [o, []]}
            else:
                s = self._state(buf, part)
                s[0] = o
                s[1] = []
        deps.discard(o)
        fd = []
        best = {}
        for d in deps:
            if d.is_dma:
                fd.append(d)
                continue
            if d.eng == eng and (not dma) and eng == "tensor":
                continue
            b = best.get(d.eng)
            if b is None or d.seq > b.seq:
                best[d.eng] = d
        fd.extend(best.values())
        for d in fd:
            d.needs_inc = True
        o.deps = fd
        o.seq = len(self.ops[eng])
        self.ops[eng].append(o)
        if not dma:
            self.last_comp[eng] = o
        return o

    def barrier(self):
        deps = [o for o in self.last_comp.values() if o is not None]
        deps += self.dma_ops[-NPOOL:]
        for d in deps:
            d.needs_inc = True
        for e in ENGS:
            o = Op()
            o.eng, o.fn, o.is_dma, o.needs_inc, o.semval, o.dma_n = e, None, False, False, None, None
            o.deps = [d for d in deps if not (d.eng == e and not d.is_dma)]
            o.seq = len(self.ops[e])
            self.ops[e].append(o)

    def T(self, fn, reads=(), writes=()):
        return self.op("tensor", fn, reads, writes)

    def V(self, fn, reads=(), writes=()):
        return self.op("vector", fn, reads, writes)

    def S(self, fn, reads=(), writes=()):
        return self.op("scalar", fn, reads, writes)

    def G(self, fn, reads=(), writes=()):
        return self.op("gpsimd", fn, reads, writes)

    def dma(self, eng, out, in_, reads=(), writes=()):
        return self.op(eng, lambda e: e.dma_start(out=out, in_=in_), reads, writes, dma=True)


    def mm(self, out, lhsT, rhs, start, stop, r, w):
        return self.op("tensor", lambda e: e.matmul(out, lhsT=lhsT, rhs=rhs, start=start, stop=stop), r, w)

    def act(self, out, in_, func, r, w, bias=None, scale=1.0):
        kw = {}
        if bias is not None:
            kw["bias"] = bias
        return self.op("scalar", lambda e: e.activation(out=out, in_=in_, func=func, scale=scale, **kw), r, w)

    def tt(self, out, in0, in1, op, r, w):
        return self.op("vector", lambda e: e.tensor_tensor(out=out, in0=in0, in1=in1, op=op), r, w)

    def ts(self, out, in0, s1, s2, op0, op1, r, w):
        if op1 is None:
            return self.op("vector", lambda e: e.tensor_scalar(out=out, in0=in0, scalar1=s1, scalar2=None, op0=op0), r, w)
        return self.op("vector", lambda e: e.tensor_scalar(out=out, in0=in0, scalar1=s1, scalar2=s2, op0=op0, op1=op1), r, w)

    def stt(self, out, in0, scalar, in1, op0, op1, r, w):
        return self.op("vector", lambda e: e.scalar_tensor_tensor(out=out, in0=in0, scalar=scalar, in1=in1, op0=op0, op1=op1), r, w)

    def cp(self, out, in_, r, w, eng="vector"):
        return self.op(eng, lambda e: e.tensor_copy(out=out, in_=in_), r, w)

    def ms(self, out, val, w):
        return self.op("vector", lambda e: e.memset(out, val), (), w)

    def recip(self, out, in_, r, w):
        return self.op("vector", lambda e: e.reciprocal(out=out, in_=in_), r, w)

    def red(self, out, in_, op, r, w):
        return self.op("vector", lambda e: e.tensor_reduce(out=out, in_=in_, axis=AX.X, op=op), r, w)

    def scan(self, out, d0, d1, r, w):
        return self.op("vector", lambda e: e.tensor_tensor_scan(out=out, data0=d0, data1=d1, initial=0.0, op0=ALU.mult, op1=ALU.add), r, w)

    def emit(self):
        nc, es = self.nc, self.es
        pool = [es.enter_context(nc.semaphore("dsem%d" % i)) for i in range(NPOOL)]
        EPOCH = 20000
        for e in ENGS:
            c = 0
            cur = None
            nep = 0
            for o in self.ops[e]:
                if o.is_dma:
                    o.semval = (pool[o.dma_n % NPOOL], 16 * (o.dma_n // NPOOL + 1))
                elif o.needs_inc:
                    if cur is None or c == EPOCH:
                        cur = es.enter_context(nc.semaphore("sem_%s_%d" % (e, nep)))
                        nep += 1
                        c = 0
                    c += 1
                    o.semval = (cur, c)
        final = self.dma_ops[-NPOOL:]
        block = es.enter_context(nc.Block())

        def run(e, eo):
            known = {}
            for o in self.ops[e]:
                need = {}
                for d in o.deps:
                    s, v = d.semval
                    k = id(s)
                    if known.get(k, 0) >= v:
                        continue
                    if k not in need or need[k][1] < v:
                        need[k] = (s, v)
                for k, (s, v) in need.items():
                    eo.wait_ge(s, v)
                    known[k] = v
                if o.fn is None:
                    continue
                ins = o.fn(eo)
                if o.is_dma:
                    ins.then_inc(o.semval[0], 16)
                elif o.needs_inc:
                    ins.then_inc(o.semval[0], 1)
            if e == "sync":
                for d in final:
                    s, v = d.semval
                    if known.get(id(s), 0) < v:
                        eo.wait_ge(s, v)
                        known[id(s)] = v

        @block.tensor
        def _(t):
            run("tensor", t)

        @block.vector
        def _(t):
            run("vector", t)

        @block.scalar
        def _(t):
            run("scalar", t)

        @block.gpsimd
        def _(t):
            run("gpsimd", t)

        @block.sync
        def _(t):
            run("sync", t)


class Arena:
    def __init__(self, tensor, n, top=False, peer=None):
        self.t, self.n, self.top = tensor, n, top
        self.off = n if top else 0
        self.peer = peer

    def alloc(self, *shape):
        size = int(np.prod(shape))
        if self.top:
            nf = (size + 1) // 2
            self.off -= nf
            ap = self.t[:, self.off:self.off + nf].bitcast(BF16)[:, 0:size]
        else:
            off = self.off
            self.off += size
            ap = self.t[:, off:off + size]
        lo = self.peer.off if self.top else self.off
        hi = self.off if self.top else self.peer.off
        assert lo <= hi, ("arena overflow", lo, hi)
        if len(shape) == 2:
            ap = ap.rearrange("p (a b) -> p a b", a=shape[0], b=shape[1])
        elif len(shape) == 3:
            ap = ap.rearrange("p (a b c) -> p a b c", a=shape[0], b=shape[1], c=shape[2])
        return ap


def bcast(ap, axis, n):
    a = [list(x) for x in ap.ap]
    a.insert(axis, [0, n])
    return bass.AP(ap.tensor, ap.offset, a)


def build(dbg=(), units=None, stages=None):
    nc = bass.Bass("TRN2", target_bir_lowering=False)
    p = Prog(nc)

    def din(name, shape, dt=F32):
        return nc.dram_tensor(name, list(shape), dt, kind="ExternalInput").ap()

    def dscr(name, shape, dt=F32):
        kind = "ExternalOutput" if name in dbg else "Internal"
        return nc.dram_tensor(name, list(shape), dt, kind=kind).ap()

    xT = din("xT", [2, D, NT])
    cT = din("cT", [128, 8, 4])
    w_mod = din("w_mod", [2, D, 3 * D])
    b_modT = din("b_modT", [2, 128, 24])
    g_preT = din("g_preT", [2, 128, 8])
    g_postT = din("g_postT", [2, 128, 8])
    w_in = din("w_in", [2, D, DIN])
    gla_wup = din("gla_wup", [2, 2, 128, 256])
    gla_b = din("gla_b", [2, 128, 4])
    gla_normS = din("gla_normS", [2, 128, 2, 128])
    att_qk = din("att_qk", [2, 128, 2])
    rw_mu = din("rw_mu", [2, 128, 18])
    rw_vec = din("rw_vec", [2, 128, 7, 4])
    rw_up = din("rw_up", [2, 4, 128, 512])
    rw_ln = din("rw_ln", [2, 128, 2, 4, 64])
    w_o = din("w_o", [2, 3, 512, D])
    w_out = din("w_out", [2, D, D])
    c_ident = din("c_ident", [128, 128])
    c_bones = din("c_bones", [128, 128])
    c_si = din("c_si", [128, 64])
    c_pmt = din("c_pmt", [128, 128])
    c_cs = din("c_cs", [128, 2, 64])
    c_masks = din("c_masks", [128, 5, 4, 128])
    c_cmask = din("c_cmask", [128, 1024])
    outT = nc.dram_tensor("outT", [2, D, NLAT], F32, kind="ExternalOutput").ap()

    X1 = dscr("X1", [2, D, NT])
    OPS_R = dscr("OPS_R", [2, 6 * 512, NT])
    OPS_G = dscr("OPS_G", [2, 3 * 256, NT])
    WC_R = dscr("WC_R", [2, 512, NCH])
    WC_G = dscr("WC_G", [2, 256, NCH])
    RVP = dscr("RVP", [1024, NT])
    RG = dscr("RG", [512, NT])
    GV = dscr("GV", [512, NT])
    GG = dscr("GG", [512, NT])
    YF = dscr("YF", [NCH, 128, 256])
    Y123 = dscr("Y123", [3, 512, NT], BF16)
    HTD = dscr("HTD", [128, 8, NT], BF16) if "HTD" in dbg else None

    NTOT = 51000
    a32t = p.es.enter_context(nc.sbuf_tensor("a32", [128, NTOT], F32))
    A32 = Arena(a32t, NTOT)
    A16 = Arena(a32t, NTOT, top=True, peer=A32)
    A32.peer = A16
    PS = [p.es.enter_context(nc.psum_tensor("ps%d" % i, [128, 512], F32)) for i in range(8)]
    BPS = [Buf("ps%d" % i) for i in range(8)]

    ident = A32.alloc(128)
    ones = A32.alloc(128)
    bones = A32.alloc(128)
    si = A32.alloc(64)
    pmt = A32.alloc(128)
    cs = A32.alloc(2, 64)
    sc = A32.alloc(8, 4)
    modv = A32.alloc(24, 4)
    A1 = A32.alloc(8, 4)
    GT = A32.alloc(8, 4)
    bmod = A32.alloc(24)
    gpre = A32.alloc(8)
    gpost = A32.alloc(8)
    glab = A32.alloc(4)
    attqk = A32.alloc(2)
    rwmu = A32.alloc(18)
    rwvec = A32.alloc(7, 4)
    omka = A32.alloc(4)
    ones16 = A16.alloc(128)
    hT = A16.alloc(8, NT)
    W32 = [A32.alloc(8, 128) for _ in range(2)]
    WB = [A16.alloc(8, 128) for _ in range(2)]
    BW32 = [Buf("w32%d" % i) for i in range(2)]
    BWB = [Buf("wb%d" % i) for i in range(2)]
    BC = Buf("consts")
    BHT = Buf("hT")
    BMOD = Buf("mod")
    BLP = Buf("layerparams")
    base32, base16 = A32.off, A16.off

    for dst, src in ((ident, c_ident), (bones, c_bones), (si, c_si), (pmt, c_pmt), (cs, c_cs), (sc, cT)):
        p.dma("sync", dst, src, writes=[BC])
    p.ms(ones, 1.0, [BC])
    p.ms(ones16, 1.0, [BC])
    p.act(sc, sc, AF.Silu, [BC], [BC])
    p.barrier()

    st = {"w": 0, "bank": 0, "ev": 0}

    def bank(lo=0, hi=4):
        k = lo + st["bank"] % (hi - lo)
        st["bank"] += 1
        return k

    def evac(dst, src, reads, writes, func=None, scale=1.0):
        st["ev"] += 1
        if func is not None or st["ev"] % 2 == 0:
            p.act(dst, src, func if func is not None else AF.Identity, reads, writes, scale=scale)
        elif scale == 1.0:
            p.cp(dst, src, reads, writes)
        else:
            p.ts(dst, src, scale, None, ALU.mult, None, reads, writes)

    def load_w(src_ap, ncols, nk=8):
        s = st["w"] % 2
        st["w"] += 1
        w32, wb = W32[s], WB[s]
        p.dma("sync", w32[:, 0:nk, 0:ncols], src_ap.rearrange("(k p) c -> p k c", p=128), writes=[BW32[s]])
        p.cp(wb[:, 0:nk, 0:ncols], w32[:, 0:nk, 0:ncols], [BW32[s]], [BWB[s]], eng="gpsimd")
        return wb, BWB[s]

    def proj(l, col0, ncols, t0, t1, handler):
        wb, bwb = load_w(w_in[l, :, col0:col0 + ncols], ncols)
        t = t0
        while t < t1:
            n = min(512, t1 - t)
            k = bank()
            for kt in range(8):
                p.mm(PS[k][0:ncols, 0:n], wb[:, kt, 0:ncols], hT[:, kt, t:t + n], kt == 0, kt == 7, [bwb, BHT], [BPS[k]])
            handler(k, t, n)
            t += n

    def rstd_from_psum(k, n, inv_n, eps, r1, r2, b1, b2):
        p.ts(r1[:, 0:n], PS[k][:, 0:n], inv_n, eps, ALU.mult, ALU.add, [BPS[k]], [b1])
        p.act(r1[:, 0:n], r1[:, 0:n], AF.Sqrt, [b1], [b1])
        p.recip(r2[:, 0:n], r1[:, 0:n], [b1], [b2])

    def phase_mod(l):
        A32.off, A16.off = base32, base16
        wm = [A32.alloc(8, 512) for _ in range(2)]
        bwm = [Buf("wm0"), Buf("wm1")]
        for dst, src in ((bmod, b_modT[l]), (gpre, g_preT[l]), (gpost, g_postT[l]), (glab, gla_b[l]), (attqk, att_qk[l]),
                         (rwmu, rw_mu[l]), (rwvec, rw_vec[l])):
            p.dma("sync", dst, src, writes=[BLP])
        p.ts(omka, rwvec[:, 5, :], -1.0, 1.0, ALU.mult, ALU.add, [BLP], [BLP])
        for c6 in range(6):
            s = c6 % 2
            p.dma("sync", wm[s], w_mod[l, :, c6 * 512:(c6 + 1) * 512].rearrange("(k p) c -> p k c", p=128), writes=[bwm[s]])
            for f4 in range(4):
                ft = c6 * 4 + f4
                for kt in range(8):
                    p.mm(PS[7][:, ft * 4:ft * 4 + 4], wm[s][:, kt, f4 * 128:(f4 + 1) * 128], sc[:, kt, :], kt == 0, kt == 7,
                         [bwm[s], BC], [BPS[7]])
        p.tt(modv, PS[7][:, 0:96].rearrange("p (a b) -> p a b", a=24, b=4), bcast(bmod, 2, 4), ALU.add, [BPS[7], BLP], [BMOD])
        p.ts(A1, modv[:, 8:16, :], 1.0, None, ALU.add, None, [BMOD], [BMOD])
        p.tt(A1, A1, bcast(gpre, 2, 4), ALU.mult, [BMOD, BLP], [BMOD])
        p.tt(GT, modv[:, 16:24, :], bcast(gpost, 2, 4), ALU.mult, [BMOD, BLP], [BMOD])
        p.barrier()

    def phase_a(bi, l):
        A32.off, A16.off = base32, base16
        src = xT[bi] if l == 0 else X1[bi]
        xin = [A32.alloc(8, 256) for _ in range(2)]
        bxin = [Buf("xin0"), Buf("xin1")]
        sq = A32.alloc(8, 256)
        xn = A32.alloc(8, 256)
        r1, r2 = A32.alloc(256), A32.alloc(256)
        bsq, bxn, b1, b2 = Buf("sq"), Buf("xn"), Buf("r1"), Buf("r2")
        for blk in range(17):
            t0 = blk * 256
            j = bi if t0 < NLAT else 2
            s = blk % 2
            p.dma("sync", xin[s], src[:, t0:t0 + 256].rearrange("(k p) t -> p k t", p=128), writes=[bxin[s]])
            p.act(sq, xin[s], AF.Square, [bxin[s]], [bsq])
            k = bank()
            for kt in range(8):
                p.mm(PS[k][:, 0:256], ones, sq[:, kt, :], kt == 0, kt == 7, [BC, bsq], [BPS[k]])
            rstd_from_psum(k, 256, 1.0 / D, 1e-6, r1, r2, b1, b2)
            p.tt(xn, xin[s], bcast(r2, 1, 8), ALU.mult, [bxin[s], b2], [bxn])
            for kt in range(8):
                p.act(hT[:, kt, t0:t0 + 256], xn[:, kt, :], AF.Identity, [bxn, BMOD], [(BHT, blk)],
                      bias=modv[:, kt, j:j + 1], scale=A1[:, kt, j:j + 1])
        if HTD is not None:
            p.dma("gpsimd", HTD, hT, reads=[BHT])
        p.barrier()

    def phase_att(bi, l):
        A32.off, A16.off = base32, base16
        need_ctx = (l == 0)
        kT = A16.alloc(2, NT)
        Vt = A16.alloc(34, 256)
        qTh = A16.alloc(NT)
        pT = [A16.alloc(512) for _ in range(3)]
        y2b = [A16.alloc(512) for _ in range(2)]
        bkT, bV, bq = Buf("kT"), Buf("V"), Buf("qTh")
        bpT = [Buf("pT%d" % i) for i in range(3)]
        by2 = [Buf("y2b0"), Buf("y2b1")]
        x32 = [A32.alloc(512) for _ in range(2)]
        bx32 = [Buf("x320"), Buf("x321")]
        sqb, r1, r2, xnb, t1b, t2b = (A32.alloc(512) for _ in range(6))
        bsq, b1, b2, bxn, bt1, bt2 = (Buf(n) for n in ("sqb", "r1", "r2", "xnb", "t1b", "t2b"))
        gb = [A32.alloc(512) for _ in range(2)]
        bgb = [Buf("gb0"), Buf("gb1")]
        rden, ob = A32.alloc(512), A32.alloc(512)
        brd, bob = Buf("rden"), Buf("ob")
        cnt = {"x": 0}

        def qk_tile(col0, gcol, dst_fn, bdst, tmax):
            def h(k, t, n):
                s = cnt["x"] % 2
                cnt["x"] += 1
                xb = x32[s]
                evac(xb[:, 0:n], PS[k][:, 0:n], [BPS[k]], [bx32[s]])
                p.act(sqb[:, 0:n], xb[:, 0:n], AF.Square, [bx32[s]], [bsq])
                k2 = bank()
                p.mm(PS[k2][:, 0:n], ones, sqb[:, 0:n], True, True, [BC, bsq], [BPS[k2]])
                rstd_from_psum(k2, n, 1.0 / 128, 1e-6, r1, r2, b1, b2)
                p.stt(xnb[:, 0:n], xb[:, 0:n], attqk[:, gcol:gcol + 1], r2[:, 0:n], ALU.mult, ALU.mult, [bx32[s], BLP, b2], [bxn])
                dst = dst_fn(t, n)
                if t >= NLAT:
                    p.cp(dst, xnb[:, 0:n], [bxn], [bdst])
                    return
                k3 = bank()
                p.mm(PS[k3][:, 0:n], pmt, xnb[:, 0:n], True, True, [BC, bxn], [BPS[k3]])
                r0 = t // 64
                nr = n // 64
                for half in (0, 1):
                    ps_ = slice(half * 64, half * 64 + 64)
                    if half == 0:
                        ctab = bcast(cs[ps_, 0, r0:r0 + nr], 2, 64)
                        stab = bcast(cs[ps_, 1, r0:r0 + nr], 2, 64)
                    else:
                        ctab = bcast(cs[ps_, 0, :], 1, nr)
                        stab = bcast(cs[ps_, 1, :], 1, nr)
                    p.tt(t1b[ps_, 0:n].rearrange("p (a b) -> p a b", a=nr, b=64), PS[k3][ps_, 0:n].rearrange("p (a b) -> p a b", a=nr, b=64),
                         stab, ALU.mult, [BPS[k3], BC], [(bt1, half)])
                    p.tt(t2b[ps_, 0:n].rearrange("p (a b) -> p a b", a=nr, b=64), xnb[ps_, 0:n].rearrange("p (a b) -> p a b", a=nr, b=64),
                         ctab, ALU.mult, [bxn, BC], [(bt2, half)])
                p.tt(dst, t1b[:, 0:n], t2b[:, 0:n], ALU.add, [bt1, bt2], [bdst])
            proj(l, col0, 128, 0, tmax, h)

        for kv in range(2):
            qk_tile(C_ATT + 512 + kv * 128, 1, (lambda t, n, kv=kv: kT[:, kv, t:t + n]), bkT, NT)
        wva, bwva = load_w(w_in[l, :, C_ATT + 768:C_ATT + 896], 128)
        wvb, bwvb = load_w(w_in[l, :, C_ATT + 896:C_ATT + 1024], 128)
        for tt in range(34):
            k = bank()
            for hv, (wv_, bw_) in enumerate(((wva, bwva), (wvb, bwvb))):
                for kt in range(8):
                    p.mm(PS[k][:, hv * 128:(hv + 1) * 128], hT[:, kt, tt * 128:(tt + 1) * 128], wv_[:, kt, :], kt == 0, kt == 7,
                         [bw_, BHT], [BPS[k]])
            evac(Vt[:, tt, :], PS[k][:, 0:256], [BPS[k]], [(bV, tt)])
        scale = 128 ** -0.5
        for hq in range(4):
            kv = hq // 2
            qk_tile(C_ATT + hq * 128, 0, (lambda t, n: qTh[:, t:t + n]), bq, NT if need_ctx else NLAT)
            wg, bwg = load_w(w_in[l, :, C_ATT + 1024 + hq * 128:C_ATT + 1024 + (hq + 1) * 128], 128)
            qblocks = [(qb * 512, 512, list(range(34))) for qb in range(8)]
            if need_ctx:
                qblocks.append((NLAT, 256, [32, 33]))
            for qi, (q0, qn, keys) in enumerate(qblocks):
                kg = bank(6, 8)
                for kt in range(8):
                    p.mm(PS[kg][:, 0:qn], wg[:, kt, :], hT[:, kt, q0:q0 + qn], kt == 0, kt == 7, [bwg, BHT], [BPS[kg]])
                gs = qi % 2
                p.act(gb[gs][:, 0:qn], PS[kg][:, 0:qn], AF.Silu, [BPS[kg]], [bgb[gs]])
                nk = len(keys)

                def qk(i, q0=q0, qn=qn, keys=keys, kv=kv):
                    sidx = keys[i]
                    kb = i % 3
                    p.mm(PS[kb][:, 0:qn], kT[:, kv, sidx * 128:(sidx + 1) * 128], qTh[:, q0:q0 + qn], True, True, [bkT, bq], [BPS[kb]])
                    p.act(pT[kb][:, 0:qn], PS[kb][:, 0:qn], AF.Exp, [BPS[kb]], [bpT[kb]], scale=scale)

                def pv(i, qn=qn, keys=keys, kv=kv, nk=nk):
                    sidx = keys[i]
                    kb = i % 3
                    p.mm(PS[4][:, 0:qn], Vt[:, sidx, kv * 128:(kv + 1) * 128], pT[kb][:, 0:qn], i == 0, i == nk - 1, [bV, bpT[kb]], [BPS[4]])
                    p.mm(PS[5][:, 0:qn], ones16, pT[kb][:, 0:qn], i == 0, i == nk - 1, [BC, bpT[kb]], [BPS[5]])
                qk(0)
                if nk > 1:
                    qk(1)
                for i in range(nk):
                    pv(i)
                    if i + 2 < nk:
                        qk(i + 2)
                p.recip(rden[:, 0:qn], PS[5][:, 0:qn], [BPS[5]], [brd])
                p.tt(ob[:, 0:qn], PS[4][:, 0:qn], rden[:, 0:qn], ALU.mult, [BPS[4], brd], [bob])
                ys = qi % 2
                p.tt(y2b[ys][:, 0:qn], ob[:, 0:qn], gb[gs][:, 0:qn], ALU.mult, [bob, bgb[gs]], [by2[ys]])
                p.dma("gpsimd", Y123[1, hq * 128:(hq + 1) * 128, q0:q0 + qn], y2b[ys][:, 0:qn], reads=[by2[ys]])
        p.barrier()

    SEGS = [(0, 1024), (1024, 2048), (2048, 3072), (3072, 4096), (4096, 4352)]

    def decay_rows_dir(L, t0, LG, bLG, sconst, d, items, rows, ops_dst, wc_dst, j, npair):
        cum, tail, cl, tl, ex, o0, o1, wcb, cmask, bcm = rows
        nchk = L // 64
        c0 = t0 // 64
        B = {n: Buf(n) for n in ("cum", "tail", "cl", "tl", "ex", "wcb")}
        bo = [Buf("o0"), Buf("o1")]
        outs = [o0, o1]
        p.scan(cum[:, 0:L], cmask[:, 0:L], LG[:, 0:L], [bcm, bLG], [B["cum"]])
        cum3 = cum[:, 0:L].rearrange("p (a b) -> p a b", a=nchk, b=64)
        tot = cum3[:, :, 63]
        p.tt(tail[:, 0:L].rearrange("p (a b) -> p a b", a=nchk, b=64), bcast(tot, 2, 64), cum3, ALU.subtract, [B["cum"]], [B["tail"]])
        p.tt(cl[:, 0:L], cum[:, 0:L], LG[:, 0:L], ALU.subtract, [B["cum"], bLG], [B["cl"]])
        p.tt(tl[:, 0:L], tail[:, 0:L], LG[:, 0:L], ALU.add, [B["tail"], bLG], [B["tl"]])
        p.act(wcb[:, 0:nchk], tot, AF.Exp, [B["cum"]], [B["wcb"]], scale=sconst)
        p.dma("gpsimd", wc_dst[d, j * 128:(j + 1) * 128, c0:c0 + nchk], wcb[:, 0:nchk], reads=[B["wcb"]])
        esrc = {"inc": (cum, B["cum"]) if d == 0 else (tl, B["tl"]),
                "exc": (cl, B["cl"]) if d == 0 else (tail, B["tail"]),
                "tail": (tail, B["tail"]) if d == 0 else (cl, B["cl"])}
        oc = 0
        for (src, bsrc, kind, mulc, opidx) in items:
            sgn = -1.0 if kind == "ninc" else 1.0
            er, ber = esrc["inc" if kind == "ninc" else kind]
            p.act(ex[:, 0:L], er[:, 0:L], AF.Exp, [ber], [B["ex"]], scale=sconst * sgn)
            o = outs[oc % 2]
            bo_ = bo[oc % 2]
            oc += 1
            p.stt(o[:, 0:L], src[:, 0:L], mulc, ex[:, 0:L], ALU.mult, ALU.mult, [bsrc, B["ex"]], [bo_])
            r0 = (opidx * npair + j) * 128
            p.dma("gpsimd", ops_dst[d, r0:r0 + 128, t0:t0 + L], o[:, 0:L], reads=[bo_])

    def phase_gla_prep(bi, l):
        A32.off, A16.off = base32, base16
        LM = 1024
        names = ("wd", "q", "k", "sg", "lg", "cum", "tail", "cl", "tl", "ex", "o0", "o1", "vg0", "vg1", "cmask")
        R = {n: A32.alloc(LM) for n in names}
        B = {n: Buf(n) for n in names}
        wcb = A32.alloc(16)
        wup = A32.alloc(2, 256)
        bwup = Buf("wup")
        p.dma("sync", R["cmask"], c_cmask, writes=[B["cmask"]])
        p.dma("sync", wup[:, 0, :], gla_wup[l, 0], writes=[bwup])
        p.dma("sync", wup[:, 1, :], gla_wup[l, 1], writes=[bwup])
        p.ms(R["wd"], 0.0, [B["wd"]])
        for (t0, t1) in SEGS:
            L = t1 - t0

            def mk_h(row, brow, t0=t0, func=None, np_=128):
                def h(k, t, n):
                    evac(row[0:np_, t - t0:t - t0 + n], PS[k][0:np_, 0:n], [BPS[k]], [brow], func=func)
                return h
            proj(l, C_GLA + 1536, 32, t0, t1, mk_h(R["wd"], B["wd"], np_=32))
            for j in range(2):
                proj(l, C_GLA + j * 128, 128, t0, t1, mk_h(R["q"], B["q"]))
                proj(l, C_GLA + 256 + j * 128, 128, t0, t1, mk_h(R["k"], B["k"]))
                for d in range(2):
                    for b0 in range(0, L, 512):
                        n = min(512, L - b0)
                        k = bank()
                        p.mm(PS[k][:, 0:n], wup[:, d, j * 128:(j + 1) * 128], R["wd"][:, b0:b0 + n], True, True, [bwup, B["wd"]], [BPS[k]])
                        p.act(R["sg"][:, b0:b0 + n], PS[k][:, 0:n], AF.Sigmoid, [BPS[k], BLP], [B["sg"]], bias=glab[:, d * 2 + j:d * 2 + j + 1])
                    p.act(R["lg"][:, 0:L], R["sg"][:, 0:L], AF.Ln, [B["sg"]], [B["lg"]])
                    items = [(R["q"], B["q"], "inc", 0.125, 0), (R["k"], B["k"], "ninc", 1.0, 1), (R["k"], B["k"], "tail", 1.0, 2)]
                    rows = (R["cum"], R["tail"], R["cl"], R["tl"], R["ex"], R["o0"], R["o1"], wcb, R["cmask"], B["cmask"])
                    decay_rows_dir(L, t0, R["lg"], B["lg"], S_GLA, d, items, rows, OPS_G, WC_G, j, 2)
            for h_ in range(4):
                for which, (col, dst, fn) in enumerate(((C_GLA + 512 + h_ * 128, GV, None), (C_GLA + 1024 + h_ * 128, GG, AF.Silu))):
                    rr, br = R["vg%d" % which], B["vg%d" % which]
                    proj(l, col, 128, t0, t1, mk_h(rr, br, func=fn))
                    p.dma("gpsimd", dst[h_ * 128:(h_ + 1) * 128, t0:t1], rr[:, 0:L], reads=[br])
        p.barrier()

    def phase_rw_prep(bi, l):
        A32.off, A16.off = base32, base16
        LM = 1024
        names = ("pe", "t1", "t2", "wdt", "ad", "r", "k", "vg", "kk", "sg", "a", "kdf", "kdb", "b", "cum", "tail", "cl", "tl",
                 "ex", "o0", "o1", "cmask")
        R = {n: A32.alloc(LM + 2) for n in names}
        B = {n: Buf(n) for n in names}
        wcb = A32.alloc(16)
        up = A32.alloc(4, 512)
        bup = Buf("up")
        sqb, r1, r2 = A32.alloc(512), A32.alloc(512), A32.alloc(512)
        bsq, b1, b2 = Buf("sqb"), Buf("r1"), Buf("r2")
        p.dma("sync", R["cmask"][:, 0:1024], c_cmask, writes=[B["cmask"]])
        for i in range(4):
            p.dma("sync", up[:, i, :], rw_up[l, i], writes=[bup])
        for (t0, t1) in SEGS:
            L = t1 - t0
            reg0, reg1 = (0, NLAT) if t0 < NLAT else (NLAT, NT)
            hl = 1 if t0 > reg0 else 0
            hr = 1 if t1 < reg1 else 0
            pe = R["pe"]

            def shifted(col0, mucol, dst, bdst, func=None, t0=t0, t1=t1, L=L, hl=hl, hr=hr):
                if hl == 0:
                    p.ms(pe[:, 0:1], 0.0, [B["pe"]])
                if hr == 0:
                    p.ms(pe[:, L + 1:L + 2], 0.0, [B["pe"]])

                def h(k, t, n):
                    o = t - (t0 - hl) + (1 - hl)
                    evac(pe[:, o:o + n], PS[k][:, 0:n], [BPS[k]], [B["pe"]])
                proj(l, col0, 128, t0 - hl, t1 + hr, h)
                p.tt(R["t1"][:, 0:L], pe[:, 0:L], pe[:, 2:L + 2], ALU.add, [B["pe"]], [B["t1"]])
                p.stt(R["t2"][:, 0:L], R["t1"][:, 0:L], 0.5, pe[:, 1:L + 1], ALU.mult, ALU.subtract, [B["t1"], B["pe"]], [B["t2"]])
                p.stt(dst[:, 0:L], R["t2"][:, 0:L], rwmu[:, mucol:mucol + 1], pe[:, 1:L + 1], ALU.mult, ALU.add, [B["t2"], B["pe"], BLP], [bdst])
                if func is not None:
                    p.act(dst[:, 0:L], dst[:, 0:L], func, [bdst], [bdst])

            shifted(C_RW + 2048, 16, R["wdt"], B["wdt"], AF.Tanh)
            shifted(C_RW + 2176, 17, R["ad"], B["ad"])
            for j in range(4):
                shifted(C_RW + j * 128, j, R["r"], B["r"])
                shifted(C_RW + 512 + j * 128, 4 + j, R["k"], B["k"])
                shifted(C_RW + 1024 + j * 128, 8 + j, R["vg"], B["vg"])
                p.dma("gpsimd", RVP[j * 128:(j + 1) * 128, t0:t1], R["vg"][:, 0:L], reads=[B["vg"]])
                shifted(C_RW + 1536 + j * 128, 12 + j, R["vg"], B["vg"], AF.Silu)
                p.dma("gpsimd", RG[j * 128:(j + 1) * 128, t0:t1], R["vg"][:, 0:L], reads=[B["vg"]])
                p.ts(R["kk"][:, 0:L], R["k"][:, 0:L], rwvec[:, 4, j:j + 1], None, ALU.mult, None, [B["k"], BLP], [B["kk"]])
                for b0 in range(0, L, 512):
                    n = min(512, L - b0)
                    p.act(sqb[:, 0:n], R["kk"][:, b0:b0 + n], AF.Square, [B["kk"]], [bsq])
                    k = bank()
                    p.mm(PS[k][:, 0:n], bones, sqb[:, 0:n], True, True, [BC, bsq], [BPS[k]])
                    rstd_from_psum(k, n, 1.0, 1e-12, r1, r2, b1, b2)
                    p.tt(R["kk"][:, b0:b0 + n], R["kk"][:, b0:b0 + n], r2[:, 0:n], ALU.mult, [B["kk"], b2], [B["kk"]])
                for d in range(2):
                    kd, bkd = (R["kdf"], B["kdf"]) if d == 0 else (R["kdb"], B["kdb"])
                    for b0 in range(0, L, 512):
                        n = min(512, L - b0)
                        k = bank()
                        p.mm(PS[k][:, 0:n], up[:, d, j * 128:(j + 1) * 128], R["wdt"][:, b0:b0 + n], True, True, [bup, B["wdt"]], [BPS[k]])
                        p.act(R["sg"][:, b0:b0 + n], PS[k][:, 0:n], AF.Sigmoid, [BPS[k], BLP], [B["sg"]], bias=rwvec[:, d, j:j + 1])
                        k = bank()
                        p.mm(PS[k][:, 0:n], up[:, 2 + d, j * 128:(j + 1) * 128], R["ad"][:, b0:b0 + n], True, True, [bup, B["ad"]], [BPS[k]])
                        p.act(R["a"][:, b0:b0 + n], PS[k][:, 0:n], AF.Sigmoid, [BPS[k], BLP], [B["a"]], bias=rwvec[:, 2 + d, j:j + 1])
                    p.ts(R["t1"][:, 0:L], R["a"][:, 0:L], rwvec[:, 5, j:j + 1], omka[:, j:j + 1], ALU.mult, ALU.add, [B["a"], BLP], [B["t1"]])
                    p.tt(kd[:, 0:L], R["t1"][:, 0:L], R["k"][:, 0:L], ALU.mult, [B["t1"], B["k"]], [bkd])
                    p.tt(R["b"][:, 0:L], R["kk"][:, 0:L], R["a"][:, 0:L], ALU.mult, [B["kk"], B["a"]], [B["b"]])
                    items = [(R["r"], B["r"], "inc", 1.0, 0), (R["kk"], B["kk"], "exc", 1.0, 1), (kd, bkd, "ninc", 1.0, 2),
                             (R["b"], B["b"], "ninc", 1.0, 3), (kd, bkd, "tail", 1.0, 4), (R["b"], B["b"], "tail", 1.0, 5)]
                    rows = (R["cum"], R["tail"], R["cl"], R["tl"], R["ex"], R["o0"], R["o1"], wcb, R["cmask"], B["cmask"])
                    decay_rows_dir(L, t0, R["sg"], B["sg"], S_RW, d, items, rows, OPS_R, WC_R, j, 4)
                p.tt(R["t1"][:, 0:L], R["kdf"][:, 0:L], R["kdb"][:, 0:L], ALU.add, [B["kdf"], B["kdb"]], [B["t1"]])
                p.stt(R["t2"][:, 0:L], R["r"][:, 0:L], rwvec[:, 6, j:j + 1], R["t1"][:, 0:L], ALU.mult, ALU.mult, [B["r"], B["t1"], BLP], [B["t2"]])
                p.dma("gpsimd", RVP[512 + j * 128:512 + (j + 1) * 128, t0:t1], R["t2"][:, 0:L], reads=[B["t2"]])
        p.barrier()

    def phase_scan(bi, l, rw):
        A32.off, A16.off = base32, base16
        NP_ = 4 if rw else 2
        DV = 64 if rw else 128
        NOPS = 6 if rw else 3
        OPS = OPS_R if rw else OPS_G
        WCD = WC_R if rw else WC_G
        W = NP_ * 128
        ND = NP_ * DV

        def v3(ap2, n=NP_):
            return ap2.rearrange("p (j c) -> p j c", j=n)
        masks = A32.alloc(5, 4, 128)
        bmask = Buf("masks")
        p.dma("sync", masks, c_masks, writes=[bmask])
        lnp = A32.alloc(2, 4, 64) if rw else A32.alloc(2, 128)
        blnp = Buf("lnp")
        p.dma("sync", lnp, rw_ln[l] if rw else gla_normS[l], writes=[blnp])
        wc = A32.alloc(NP_, NCH)
        bwc = Buf("wc")
        XB = [A32.alloc(NOPS, NP_, 128) for _ in range(2)]
        bXB = [Buf("xb0"), Buf("xb1")]
        VX = [A32.alloc(2 if rw else 1, 4, 128) for _ in range(2)]
        bVX = [Buf("vx0"), Buf("vx1")]
        for i in range(2):
            p.ms(XB[i], 0.0, [bXB[i]])
            p.ms(VX[i], 0.0, [bVX[i]])

        def mat(n):
            return [A32.alloc(NP_, 128) for _ in range(n)]
        BkT = mat(2)
        bBkT = [Buf("bkt0"), Buf("bkt1")]
        KTt = mat(2)
        bKTt = [Buf("ktt0"), Buf("ktt1")]
        Vs = [A32.alloc(NP_, DV) for _ in range(2)]
        bVs = [Buf("vs0"), Buf("vs1")]
        Ys = [A32.alloc(NP_, DV) for _ in range(2)]
        bYs = [Buf("ys0"), Buf("ys1")]
        Sb = [A32.alloc(NP_, DV) for _ in range(2)]
        bS = [Buf("s0"), Buf("s1")]
        if rw:
            AkT, BbT, TT, BTt = mat(2), mat(2), mat(2), mat(2)
            bAkT, bBbT, bTT, bBTt = ([Buf(n + "0"), Buf(n + "1")] for n in ("akt", "bbt", "tt", "btt"))
            Mm, MmT = mat(1)[0], mat(1)[0]
            bM, bMT = Buf("M"), Buf("MT")
            Rr, RrT, Pp, PpT = mat(2), mat(2), mat(2), mat(2)
            bR, bRT, bP, bPT = ([Buf(n + "0"), Buf(n + "1")] for n in ("R", "RT", "P", "PT"))
            Xs, Us = A32.alloc(NP_, DV), A32.alloc(NP_, DV)
            bXs, bUs = Buf("xs"), Buf("us")
            bon = A32.alloc(4)
            bbon = Buf("bon")
        yf = [A32.alloc(NP_, DV) for _ in range(2)]
        byf = [Buf("yf0"), Buf("yf1")]
        gg = [A32.alloc(4, 64) for _ in range(2)]
        bgg = [Buf("gg0"), Buf("gg1")]
        ysum, ysq, ytmp = A32.alloc(NP_, DV), A32.alloc(NP_, DV), A32.alloc(NP_, DV)
        bysum, bysq, bytmp = Buf("ysum"), Buf("ysq"), Buf("ytmp")
        st1, st2, st3 = A32.alloc(4), A32.alloc(4), A32.alloc(4)
        bst1, bst2, bst3 = Buf("st1"), Buf("st2"), Buf("st3")
        yblk = A32.alloc(NP_, 128)
        byblk = Buf("yblk")
        yo = [A16.alloc(4, 64) for _ in range(2)]
        byo = [Buf("yo0"), Buf("yo1")]
        if rw:
            p.ms(yblk, 0.0, [byblk])
        ybr = 2 if rw else 0
        idm = masks[:, 4, 0:NP_, :]

        for d in range(2):
            p.dma("sync", wc, WCD[d].rearrange("(j p) c -> p j c", p=128), writes=[bwc])
            order = ([64, 65, 66, 67] + list(range(64))) if d == 0 else ([67, 66, 65, 64] + list(range(63, -1, -1)))
            M_INC, M_STR, M_STRT = (0, 1, 3) if d == 0 else (2, 3, 1)
            p.ms(Sb[0], 0.0, [bS[0]])
            opsv = OPS[d].rearrange("(o p) t -> p o t", p=128)
            nrow = NOPS * NP_
            for ci, c in enumerate(order):
                s = ci % 2
                tc0 = c * 64
                xb, vx = XB[s], VX[s]
                xbf = xb.rearrange("p o j c -> p (o j) c")
                p.dma("sync", xbf[0:64, :, 0:64], opsv[0:64, 0:nrow, tc0:tc0 + 64], writes=[bXB[s]])
                p.dma("sync", xbf[64:128, :, 64:128], opsv[64:128, 0:nrow, tc0:tc0 + 64], writes=[bXB[s]])
                if rw:
                    vv = RVP.rearrange("(o p) t -> p o t", p=128)
                    vxf = vx.rearrange("p o j c -> p (o j) c")
                    p.dma("sync", vxf[0:64, :, 0:64], vv[0:64, :, tc0:tc0 + 64], writes=[bVX[s]])
                    p.dma("sync", vxf[64:128, :, 64:128], vv[64:128, :, tc0:tc0 + 64], writes=[bVX[s]])
                else:
                    vv = GV.rearrange("(h p) t -> p h t", p=128)
                    p.dma("sync", vx[:, 0, 0::2, 0:64], vv[:, 0::2, tc0:tc0 + 64], writes=[bVX[s]])
                    p.dma("sync", vx[:, 0, 1::2, 64:128], vv[:, 1::2, tc0:tc0 + 64], writes=[bVX[s]])
                if d == 1:
                    p.dma("sync", yf[s], v3(YF[c]), writes=[byf[s]])
                    gsrc = (RG if rw else GG).rearrange("(h p) t -> p h t", p=128)
                    p.dma("sync", gg[s], gsrc[:, :, tc0:tc0 + 64], writes=[bgg[s]])
                QW = xb[:, 0]
                if rw:
                    KKW, KI, BI, KT, BT = xb[:, 1], xb[:, 2], xb[:, 3], xb[:, 4], xb[:, 5]
                else:
                    KI, KT = xb[:, 1], xb[:, 2]

                def blockmm(lhs, blhs, rhs, brhs, ncol=128):
                    k = bank()
                    for j in range(NP_):
                        p.mm(PS[k][:, j * ncol:(j + 1) * ncol], lhs[:, j, :], rhs[:, j, :] if rhs.ndim == 3 else rhs, True, True,
                             [blhs, brhs], [BPS[k]])
                    return k, v3(PS[k][:, 0:NP_ * ncol])

                def gram(lhs, rhs, dst, bdst, mi, neg=False):
                    k, psv = blockmm(lhs, bXB[s], rhs, bXB[s])
                    mk = masks[:, mi, 0:NP_, :]
                    if neg:
                        p.stt(dst, psv, -1.0, mk, ALU.mult, ALU.mult, [BPS[k], bmask], [bdst])
                    else:
                        p.tt(dst, psv, mk, ALU.mult, [BPS[k], bmask], [bdst])

                gram(KI, QW, BkT[s], bBkT[s], M_INC)
                if rw:
                    gram(KI, KKW, AkT[s], bAkT[s], M_STR)
                    gram(BI, QW, BbT[s], bBbT[s], M_INC, neg=True)
                    gram(BI, KKW, Mm, bM, M_STR)
                    gram(KKW, BI, MmT, bMT, M_STRT)
                    p.tt(Rr[0], idm, Mm, ALU.subtract, [bmask, bM], [bR[0]])
                    p.tt(RrT[0], idm, MmT, ALU.subtract, [bmask, bMT], [bRT[0]])
                    curP, curPT, bcP, bcPT = Mm, MmT, bM, bMT
                    for lev in range(5):
                        a, b_ = lev % 2, (lev + 1) % 2
                        last = (lev == 4)
                        k, psv = blockmm(curPT, bcPT, curP, bcP)
                        evac(Pp[a], psv, [BPS[k]], [bP[a]])
                        if not last:
                            k, psv = blockmm(curP, bcP, curPT, bcPT)
                            evac(PpT[a], psv, [BPS[k]], [bPT[a]])
                        rdst, brdst = (TT[s], bTT[s]) if last else (Rr[b_], bR[b_])
                        k, psv = blockmm(RrT[a], bRT[a], Pp[a], bP[a])
                        p.tt(rdst, psv, Rr[a], ALU.add, [BPS[k], bR[a]], [brdst])
                        if not last:
                            k, psv = blockmm(Pp[a], bP[a], RrT[a], bRT[a])
                            p.tt(RrT[b_], psv, RrT[a], ALU.add, [BPS[k], bRT[a]], [bRT[b_]])
                        curP, curPT, bcP, bcPT = Pp[a], PpT[a], bP[a], bPT[a]
                if rw:
                    k, psv = blockmm(vx[:, 0], bVX[s], si, BC, ncol=64)
                else:
                    k = bank()
                    for j in range(NP_):
                        for hh in range(2):
                            p.mm(PS[k][:, j * 128:(j + 1) * 128], vx[:, 0, 2 * j + hh, :], ident, hh == 0, hh == 1, [bVX[s], BC], [BPS[k]])
                    psv = v3(PS[k][:, 0:ND])
                evac(Vs[s], psv, [BPS[k]], [bVs[s]])
                k, psv = blockmm(KT, bXB[s], ident, BC)
                evac(KTt[s], psv, [BPS[k]], [bKTt[s]])
                if rw:
                    k, psv = blockmm(BT, bXB[s], ident, BC)
                    evac(BTt[s], psv, [BPS[k]], [bBTt[s]], scale=-1.0)
                    if d == 1:
                        k, psv = blockmm(vx[:, 1], bVX[s], ones[:, 0:2], BC, ncol=2)
                        p.cp(bon, psv[:, :, 0], [BPS[k]], [bbon])
                S0, bS0 = Sb[ci % 2], bS[ci % 2]
                S1, bS1 = Sb[(ci + 1) % 2], bS[(ci + 1) % 2]
                if rw:
                    for j in range(NP_):
                        p.mm(PS[4][:, j * DV:(j + 1) * DV], KKW[:, j, :], S0[:, j, :], True, False, [bXB[s], bS0], [BPS[4]])
                        p.mm(PS[4][:, j * DV:(j + 1) * DV], AkT[s][:, j, :], Vs[s][:, j, :], False, True, [bAkT[s], bVs[s]], [BPS[4]])
                    p.cp(Xs, v3(PS[4][:, 0:ND]), [BPS[4]], [bXs])
                    for j in range(NP_):
                        p.mm(PS[5][:, j * DV:(j + 1) * DV], TT[s][:, j, :], Xs[:, j, :], True, True, [bTT[s], bXs], [BPS[5]])
                    p.cp(Us, v3(PS[5][:, 0:ND]), [BPS[5]], [bUs])
                for j in range(NP_):
                    p.mm(PS[6][:, j * DV:(j + 1) * DV], QW[:, j, :], S0[:, j, :], True, False, [bXB[s], bS0], [BPS[6]])
                    p.mm(PS[6][:, j * DV:(j + 1) * DV], BkT[s][:, j, :], Vs[s][:, j, :], False, not rw, [bBkT[s], bVs[s]], [BPS[6]])
                    if rw:
                        p.mm(PS[6][:, j * DV:(j + 1) * DV], BbT[s][:, j, :], Us[:, j, :], False, True, [bBbT[s], bUs], [BPS[6]])
                for j in range(NP_):
                    p.mm(PS[7][:, j * DV:(j + 1) * DV], KTt[s][:, j, :], Vs[s][:, j, :], True, not rw, [bKTt[s], bVs[s]], [BPS[7]])
                    if rw:
                        p.mm(PS[7][:, j * DV:(j + 1) * DV], BTt[s][:, j, :], Us[:, j, :], False, True, [bBTt[s], bUs], [BPS[7]])
                p.tt(S1, S0, bcast(wc[:, :, c], 2, DV), ALU.mult, [bS0, bwc], [bS1])
                p.tt(S1, S1, v3(PS[7][:, 0:ND]), ALU.add, [bS1, BPS[7]], [bS1])
                ypv = v3(PS[6][:, 0:ND])
                if d == 0:
                    p.act(Ys[s], ypv, AF.Identity, [BPS[6]], [bYs[s]])
                    p.dma("gpsimd", v3(YF[c]), Ys[s], reads=[bYs[s]])
                    continue
                p.tt(ysum, ypv, yf[s], ALU.add, [BPS[6], byf[s]], [bysum])
                if rw:
                    p.red(st1[:, 0:NP_], ysum, ALU.add, [bysum], [bst1])
                    p.ts(st1[:, 0:NP_], st1[:, 0:NP_], 1.0 / DV, None, ALU.mult, None, [bst1], [bst1])
                    p.tt(ysum, ysum, bcast(st1[:, 0:NP_], 2, DV), ALU.subtract, [bysum, bst1], [bysum])
                p.tt(ysq, ysum, ysum, ALU.mult, [bysum], [bysq])
                p.red(st2[:, 0:NP_], ysq, ALU.add, [bysq], [bst2])
                eps = 64e-5 if rw else 1e-6
                p.ts(st2[:, 0:NP_], st2[:, 0:NP_], 1.0 / DV, eps, ALU.mult, ALU.add, [bst2], [bst2])
                p.act(st2[:, 0:NP_], st2[:, 0:NP_], AF.Sqrt, [bst2], [bst2])
                p.recip(st3[:, 0:NP_], st2[:, 0:NP_], [bst2], [bst3])
                p.tt(ysum, ysum, bcast(st3[:, 0:NP_], 2, DV), ALU.mult, [bysum, bst3], [bysum])
                if rw:
                    p.tt(ysum, ysum, lnp[:, 0], ALU.mult, [bysum, blnp], [bysum])
                    p.tt(ysum, ysum, lnp[:, 1], ALU.add, [bysum, blnp], [bysum])
                    p.tt(ytmp, Vs[s], bcast(bon, 2, DV), ALU.mult, [bVs[s], bbon], [bytmp])
                    for hp in range(2):
                        ps_ = slice(hp * 64, hp * 64 + 64)
                        p.tt(yblk[ps_, :, hp * 64:(hp + 1) * 64], ysum[ps_], ytmp[ps_], ALU.add, [bysum, bytmp], [(byblk, hp)])
                    k, fv = blockmm(yblk, byblk, si, BC, ncol=64)
                else:
                    p.tt(ysum, ysum, lnp, ALU.mult, [bysum, blnp], [bysum])
                    k, psv = blockmm(ysum, bysum, ident, BC)
                    fv = PS[k][:, 0:256].rearrange("p (h c) -> p h c", h=4)
                p.tt(yo[s], fv, gg[s], ALU.mult, [BPS[k], bgg[s]], [byo[s]])
                p.dma("gpsimd", Y123[ybr].rearrange("(h p) t -> p h t", p=128)[:, :, tc0:tc0 + 64], yo[s], reads=[byo[s]])
            p.barrier()

    def phase_c(bi, l):
        A32.off, A16.off = base32, base16
        src = xT[bi] if l == 0 else X1[bi]
        tmax = NT if l == 0 else NLAT
        yb = [A16.alloc(3, 4, 512) for _ in range(2)]
        byb = [Buf("yb0"), Buf("yb1")]
        mT = A16.alloc(8, 512)
        bmT = Buf("mT")
        sg = [A32.alloc(512) for _ in range(2)]
        bsg = [Buf("sg0"), Buf("sg1")]
        macc, mtmp = A32.alloc(512), A32.alloc(512)
        bmacc, bmtmp = Buf("macc"), Buf("mtmp")
        o2 = A32.alloc(8, 512)
        bo2 = Buf("o2")
        xb_ = A32.alloc(8, 512)
        bxb = Buf("xblk")
        sqb, r1, r2 = A32.alloc(512), A32.alloc(512), A32.alloc(512)
        bsq, b1, b2 = Buf("sqb"), Buf("r1"), Buf("r2")
        yv = Y123.rearrange("b (k p) t -> p b k t", p=128)
        t0 = 0
        bix = 0
        while t0 < tmax:
            n = min(512, tmax - t0)
            j = bi if t0 < NLAT else 2
            s = bix % 2
            bix += 1
            for br in range(3):
                p.dma("sync", yb[s][:, br, :, 0:n], yv[:, br, :, t0:t0 + n], writes=[byb[s]])
            p.dma("sync", xb_[:, :, 0:n], src[:, t0:t0 + n].rearrange("(k p) t -> p k t", p=128), writes=[bxb])
            for dt in range(8):
                for br in range(3):
                    wg, bwg = load_w(w_in[l, :, C_MG + br * 1024 + dt * 128:C_MG + br * 1024 + (dt + 1) * 128], 128)
                    kg = bank()
                    for kt in range(8):
                        p.mm(PS[kg][:, 0:n], wg[:, kt, :], hT[:, kt, t0:t0 + n], kt == 0, kt == 7, [bwg, BHT], [BPS[kg]])
                    ss = br % 2
                    p.act(sg[ss][:, 0:n], PS[kg][:, 0:n], AF.Sigmoid, [BPS[kg]], [bsg[ss]])
                    wo, bwo = load_w(w_o[l, br, :, dt * 128:(dt + 1) * 128], 128, nk=4)
                    ko = bank()
                    for kt in range(4):
                        p.mm(PS[ko][:, 0:n], wo[:, kt, :], yb[s][:, br, kt, 0:n], kt == 0, kt == 3, [bwo, byb[s]], [BPS[ko]])
                    if br == 0:
                        p.tt(macc[:, 0:n], PS[ko][:, 0:n], sg[ss][:, 0:n], ALU.mult, [BPS[ko], bsg[ss]], [bmacc])
                    else:
                        p.tt(mtmp[:, 0:n], PS[ko][:, 0:n], sg[ss][:, 0:n], ALU.mult, [BPS[ko], bsg[ss]], [bmtmp])
                        if br == 1:
                            p.tt(macc[:, 0:n], macc[:, 0:n], mtmp[:, 0:n], ALU.add, [bmacc, bmtmp], [bmacc])
                        else:
                            p.tt(mT[:, dt, 0:n], macc[:, 0:n], mtmp[:, 0:n], ALU.add, [bmacc, bmtmp], [(bmT, dt)])
            for do in range(8):
                wq, bwq = load_w(w_out[l, :, do * 128:(do + 1) * 128], 128)
                k = bank()
                for kt in range(8):
                    p.mm(PS[k][:, 0:n], wq[:, kt, :], mT[:, kt, 0:n], kt == 0, kt == 7, [bwq, bmT], [BPS[k]])
                evac(o2[:, do, 0:n], PS[k][:, 0:n], [BPS[k]], [(bo2, do)])
            k = bank(4, 6)
            for do in range(8):
                p.act(sqb[:, 0:n], o2[:, do, 0:n], AF.Square, [(bo2, do)], [bsq])
                p.mm(PS[k][:, 0:n], ones, sqb[:, 0:n], do == 0, do == 7, [BC, bsq], [BPS[k]])
            rstd_from_psum(k, n, 1.0 / D, 1e-6, r1, r2, b1, b2)
            for do in range(8):
                p.tt(o2[:, do, 0:n], o2[:, do, 0:n], r2[:, 0:n], ALU.mult, [(bo2, do), b2], [(bo2, do)])
                p.stt(o2[:, do, 0:n], o2[:, do, 0:n], GT[:, do, j:j + 1], xb_[:, do, 0:n], ALU.mult, ALU.add, [(bo2, do), BMOD, bxb], [(bo2, do)])
            dst = X1[bi] if l == 0 else outT[bi]
            p.dma("gpsimd", dst[:, t0:t0 + n].rearrange("(k p) t -> p k t", p=128), o2[:, :, 0:n], reads=[bo2])
            t0 += n
        p.barrier()

    allst = ("a", "att", "gla", "rw", "c")
    stages = allst if stages is None else stages
    units = [(bi, l) for l in range(2) for bi in range(2)] if units is None else units
    cur_l = None
    for (bi, l) in sorted(units, key=lambda u: (u[1], u[0])):
        if l != cur_l:
            phase_mod(l)
            cur_l = l
        if "a" in stages:
            phase_a(bi, l)
        if "att" in stages:
            phase_att(bi, l)
        if "gla" in stages:
            phase_gla_prep(bi, l)
            phase_scan(bi, l, False)
        if "rw" in stages:
            phase_rw_prep(bi, l)
            phase_scan(bi, l, True)
        if "c" in stages:
            phase_c(bi, l)
    p.emit()
    p.es.close()
    p.stats = {e: len(p.ops[e]) for e in ENGS}
    _STATS.update(p.stats)
    return nc


def _consts():
    ident = np.eye(128, dtype=np.float32)
    bones = np.zeros((128, 128), np.float32)
    bones[:64, :64] = 1
    bones[64:, 64:] = 1
    si = np.concatenate([np.eye(64, dtype=np.float32)] * 2, 0)
    pm = np.zeros((128, 128), np.float32)
    for d in range(128):
        if (d % 64) < 32:
            pm[d, d + 32] = -1.0
        else:
            pm[d, d - 32] = 1.0
    pmt = np.ascontiguousarray(pm.T)
    inv = (10000.0 ** (-np.arange(32, dtype=np.float32) / 32)).astype(np.float32)
    pos = np.arange(64, dtype=np.float32)
    ang = (pos[None, :] * inv[np.arange(128) % 32][:, None]).astype(np.float32)
    cs = np.stack([np.cos(ang), np.sin(ang)], 1).astype(np.float32)
    r = np.arange(128)
    same = (r[:, None] // 64) == (r[None, :] // 64)
    rr, cc = r[:, None] % 64, r[None, :] % 64
    pats = [same & (rr <= cc), same & (rr < cc), same & (rr >= cc), same & (rr > cc), np.eye(128, dtype=bool)]
    masks = np.stack([np.repeat(m.astype(np.float32)[:, None, :], 4, 1) for m in pats], 1)
    cmask = np.ones((128, 1024), np.float32)
    cmask[:, ::64] = 0
    return dict(c_ident=ident, c_bones=bones, c_si=si, c_pmt=pmt, c_cs=np.ascontiguousarray(cs), c_masks=np.ascontiguousarray(masks), c_cmask=cmask)


def _pcol(v, nt):
    return np.ascontiguousarray(np.asarray(v, np.float32).reshape(nt, 128).T)


def _prep_shared(inp):
    L = 2
    f = lambda k: np.asarray(inp[k], np.float32)
    sh = dict(_consts())
    sh["w_mod"] = np.ascontiguousarray(f("w_mod"))
    sh["w_in"] = np.ascontiguousarray(f("w_in"))
    sh["w_out"] = np.ascontiguousarray(f("w_out"))
    sh["w_o"] = np.ascontiguousarray(np.stack([f("w_o_gla"), f("w_o_att"), f("w_o_rwkv")], 1))
    sh["b_modT"] = np.stack([_pcol(f("b_mod")[l], 24) for l in range(L)])
    sh["g_preT"] = np.stack([_pcol(f("g_pre")[l], 8) for l in range(L)])
    sh["g_postT"] = np.stack([_pcol(f("g_post")[l], 8) for l in range(L)])
    gw = np.zeros((L, 2, 128, 256), np.float32)
    gw[:, 0, 0:16] = f("gla_wup_f")
    gw[:, 1, 16:32] = f("gla_wup_b")
    sh["gla_wup"] = gw
    sh["gla_b"] = np.stack([np.concatenate([_pcol(f("gla_b_f")[l], 2), _pcol(f("gla_b_b")[l], 2)], 1) for l in range(L)])
    gn = f("gla_norm")
    hp = np.arange(128) // 64
    sh["gla_normS"] = np.ascontiguousarray(np.stack([np.stack([gn[l][2 * j + hp] for j in range(2)], 1) for l in range(L)]))
    sh["att_qk"] = np.ascontiguousarray(np.stack([np.stack([f("att_qnorm")[l], f("att_knorm")[l]], 1) for l in range(L)]))
    sh["rw_mu"] = np.stack([_pcol(f("rwkv_mu")[l], 18) for l in range(L)])
    kinds = ["rwkv_w0_f", "rwkv_w0_b", "rwkv_a0_f", "rwkv_a0_b", "rwkv_kk", "rwkv_ka"]
    rv = []
    for l in range(L):
        cols = [_pcol(f(k)[l], 4) for k in kinds] + [_pcol(f("rwkv_rk")[l].reshape(-1), 4)]
        rv.append(np.stack(cols, 1))
    sh["rw_vec"] = np.ascontiguousarray(np.stack(rv))
    ru = np.zeros((L, 4, 128, 512), np.float32)
    ru[:, 0, 0:64] = f("rwkv_wup_f")
    ru[:, 1, 64:128] = f("rwkv_wup_b")
    ru[:, 2, 0:64] = f("rwkv_aup_f")
    ru[:, 3, 64:128] = f("rwkv_aup_b")
    sh["rw_up"] = ru
    lg, lb = f("rwkv_ln_g"), f("rwkv_ln_b")
    sh["rw_ln"] = np.ascontiguousarray(np.stack([np.stack([np.stack([z[l][2 * j + hp] for j in range(4)], 1) for z in (lg, lb)], 1)
                                                 for l in range(L)]))
    return sh


def _prep_core(inp, i):
    x, ctx, c, c_ctx = (np.asarray(inp[k], np.float32) for k in ("x", "ctx", "c", "c_ctx"))
    xT = np.stack([np.concatenate([x[b].T, ctx[b].T], 1) for b in (2 * i, 2 * i + 1)])
    cc = np.zeros((4, 1024), np.float32)
    cc[0], cc[1], cc[2] = c[2 * i], c[2 * i + 1], c_ctx
    cT = np.ascontiguousarray(cc.reshape(4, 8, 128).transpose(2, 1, 0))
    return dict(xT=np.ascontiguousarray(xT), cT=cT)


_NC_CACHE = {}


def kernel(**inputs):
    if "nc" not in _NC_CACHE:
        _NC_CACHE["nc"] = build()
    nc = _NC_CACHE["nc"]
    sh = _prep_shared(inputs)
    in_maps = []
    for i in range(8):
        m = dict(sh)
        m.update(_prep_core(inputs, i))
        in_maps.append(m)
    res = run_bass_kernel_spmd(nc, in_maps, core_ids=list(range(8)))
    out = np.empty((16, NLAT, D), np.float32)
    for i in range(8):
        o = res.results[i]["outT"]
        for b in range(2):
            out[2 * i + b] = o[b].T
    return out
```
